# Optimizing a Trainium2 kernel written in Bass

```python
import math
import jax, jax.numpy as jnp
from jax import lax
import numpy as np

D_MODEL = 2048
BATCH = 4
SEQ = 2048
DEPTH = 1

MLA_HEADS = 8
QK_NOPE = 128
QK_ROPE = 64
V_HEAD = 128
Q_LORA = 512
KV_LORA = 512
ROPE_THETA = 10000.0
SWA_Q_HEADS = 16
SWA_KV_HEADS = 2
SWA_GROUP = SWA_Q_HEADS // SWA_KV_HEADS
SWA_HEAD_DIM = 64
WINDOW = 128
BLOCK = 128
REL_BUCKETS = 32
REL_MAX_DIST = 128
MLP_HIDDEN = 4 * D_MODEL
EPS = 1e-6
MLA_WIDTH = MLA_HEADS * V_HEAD
SWA_WIDTH = SWA_Q_HEADS * SWA_HEAD_DIM
MIX_WIDTH = MLA_WIDTH + SWA_WIDTH
SWA_KV_WIDTH = SWA_KV_HEADS * SWA_HEAD_DIM
IN_WIDTH = Q_LORA + KV_LORA + QK_ROPE + SWA_WIDTH + 2 * SWA_KV_WIDTH
IN_SPLIT_POINTS = (
    Q_LORA,
    Q_LORA + KV_LORA,
    Q_LORA + KV_LORA + QK_ROPE,
    Q_LORA + KV_LORA + QK_ROPE + SWA_WIDTH,
    Q_LORA + KV_LORA + QK_ROPE + SWA_WIDTH + SWA_KV_WIDTH,
)

kernel_name = "hymba_mla_swa_sink_t5_relu2"


def rms_norm(x, g):
    xf = x.astype(jnp.float32)
    xf = xf * lax.rsqrt(jnp.mean(xf * xf, axis=-1, keepdims=True) + EPS)
    return (xf * g.astype(jnp.float32)).astype(x.dtype)


def rope_angles(positions):
    inv = 1.0 / (ROPE_THETA ** (jnp.arange(0, QK_ROPE, 2, dtype=jnp.float32) / QK_ROPE))
    ang = positions.astype(jnp.float32)[..., None] * inv
    return jnp.cos(ang), jnp.sin(ang)


def apply_rope(x, cos, sin):
    xf = x.astype(jnp.float32)
    x1, x2 = xf[..., : QK_ROPE // 2], xf[..., QK_ROPE // 2:]
    out = jnp.concatenate([x1 * cos - x2 * sin, x2 * cos + x1 * sin], axis=-1)
    return out.astype(x.dtype)


def t5_bucket(dist):
    n = jnp.maximum(dist, 0)
    max_exact = REL_BUCKETS // 2
    n_safe = jnp.maximum(n, 1).astype(jnp.float32)
    large = max_exact + (jnp.log(n_safe / max_exact) / math.log(REL_MAX_DIST / max_exact)
                         * (REL_BUCKETS - max_exact)).astype(jnp.int32)
    large = jnp.minimum(large, REL_BUCKETS - 1)
    return jnp.where(n < max_exact, n, large)


def mla_attention(c_q, c_kv, k_rope_raw, cos, sin, q_a_norm, w_q_b, kv_a_norm, w_kv_b):
    B, S, _ = c_q.shape
    nb = S // BLOCK
    q = (rms_norm(c_q, q_a_norm) @ w_q_b).reshape(B, S, MLA_HEADS, QK_NOPE + QK_ROPE)
    q_nope = q[..., :QK_NOPE]
    q_rope = apply_rope(q[..., QK_NOPE:], cos[:, :, None, :], sin[:, :, None, :])
    k_rope = apply_rope(k_rope_raw, cos, sin)
    kv = (rms_norm(c_kv, kv_a_norm) @ w_kv_b).reshape(B, S, MLA_HEADS, QK_NOPE + V_HEAD)
    k_nope, v = kv[..., :QK_NOPE], kv[..., QK_NOPE:]
    qn_blocks = q_nope.reshape(B, nb, BLOCK, MLA_HEADS, QK_NOPE).transpose(1, 0, 2, 3, 4)
    qr_blocks = q_rope.reshape(B, nb, BLOCK, MLA_HEADS, QK_ROPE).transpose(1, 0, 2, 3, 4)
    key_idx = jnp.arange(S)
    scale = (QK_NOPE + QK_ROPE) ** -0.5

    def block_fn(args):
        qn, qr, blk = args
        s = (jnp.einsum('bqhd,bkhd->bhqk', qn, k_nope)
             + jnp.einsum('bqhr,bkr->bhqk', qr, k_rope)).astype(jnp.float32) * scale
        q_idx = blk * BLOCK + jnp.arange(BLOCK)
        causal = key_idx[None, :] <= q_idx[:, None]
        s = jnp.where(causal, s, -jnp.inf)
        p = jax.nn.softmax(s, axis=-1).astype(v.dtype)
        return jnp.einsum('bhqk,bkhd->bqhd', p, v)

    out = lax.map(block_fn, (qn_blocks, qr_blocks, jnp.arange(nb)))
    return out.transpose(1, 0, 2, 3, 4).reshape(B, S, MLA_WIDTH)


def _band(t):
    prev = jnp.concatenate([jnp.zeros_like(t[:, :1]), t[:, :-1]], axis=1)
    return jnp.concatenate([prev, t], axis=2)


def swa_sink_attention(q, k, v, positions, rel_bias, sinks):
    B, S, _ = q.shape
    nb = S // BLOCK
    q = q.reshape(B, nb, BLOCK, SWA_KV_HEADS, SWA_GROUP, SWA_HEAD_DIM)
    k_band = _band(k.reshape(B, nb, BLOCK, SWA_KV_HEADS, SWA_HEAD_DIM))
    v_band = _band(v.reshape(B, nb, BLOCK, SWA_KV_HEADS, SWA_HEAD_DIM))
    pos = positions.reshape(B, nb, BLOCK)
    pos_band = _band(pos)
    s = jnp.einsum('bnqhgd,bnkhd->bnhgqk', q, k_band).astype(jnp.float32) * (SWA_HEAD_DIM ** -0.5)
    bucket = t5_bucket(pos[:, :, :, None] - pos_band[:, :, None, :])
    bias = rel_bias.astype(jnp.float32)[bucket]
    bias = bias.reshape(B, nb, BLOCK, 2 * BLOCK, SWA_KV_HEADS, SWA_GROUP).transpose(0, 1, 4, 5, 2, 3)
    s = s + bias
    qi = jnp.arange(BLOCK)[:, None] + BLOCK
    kj = jnp.arange(2 * BLOCK)[None, :]
    d = qi - kj
    in_window = (d >= 0) & (d < WINDOW)
    key_global = jnp.arange(nb)[:, None, None] * BLOCK + kj[None] - BLOCK
    valid = in_window[None] & (key_global >= 0)
    s = jnp.where(valid[None, :, None, None], s, -jnp.inf)
    sink = sinks.astype(jnp.float32).reshape(SWA_KV_HEADS, SWA_GROUP)[None, None, :, :, None, None]
    m = jnp.maximum(jnp.max(s, axis=-1, keepdims=True), sink)
    e = jnp.exp(s - m)
    p = e / (jnp.sum(e, axis=-1, keepdims=True) + jnp.exp(sink - m))
    out = jnp.einsum('bnhgqk,bnkhd->bnqhgd', p.astype(v_band.dtype), v_band)
    return out.reshape(B, S, SWA_WIDTH)


def setup_inputs(seed: int = 0) -> dict:
    key = jax.random.key(seed)
    ks = jax.random.split(key, 16)
    f32 = jnp.float32

    def w(k, shape, fan_in):
        return jax.random.normal(k, shape, f32) * (fan_in ** -0.5)

    def gain(k, shape):
        return 1.0 + 0.05 * jax.random.normal(k, shape, f32)

    x = jax.random.normal(ks[0], (BATCH, SEQ, D_MODEL), f32)
    offset = jax.random.randint(ks[1], (BATCH, 1), 0, 1024, dtype=jnp.int32)
    positions = offset + jnp.arange(SEQ, dtype=jnp.int32)[None, :]
    return {
        "x": x,
        "positions": positions,
        "rel_bias": 0.5 * jax.random.normal(ks[2], (REL_BUCKETS, SWA_Q_HEADS), f32),
        "attn_norm": gain(ks[3], (DEPTH, D_MODEL)),
        "w_in": w(ks[4], (DEPTH, D_MODEL, IN_WIDTH), D_MODEL),
        "q_a_norm": gain(ks[5], (DEPTH, Q_LORA)),
        "w_q_b": w(ks[6], (DEPTH, Q_LORA, MLA_HEADS * (QK_NOPE + QK_ROPE)), Q_LORA),
        "kv_a_norm": gain(ks[7], (DEPTH, KV_LORA)),
        "w_kv_b": w(ks[8], (DEPTH, KV_LORA, MLA_HEADS * (QK_NOPE + V_HEAD)), KV_LORA),
        "sinks": 0.5 * jax.random.normal(ks[9], (DEPTH, SWA_Q_HEADS), f32),
        "w_out": w(ks[10], (DEPTH, MIX_WIDTH, D_MODEL), MIX_WIDTH),
        "mlp_norm": gain(ks[11], (DEPTH, D_MODEL)),
        "w_up": w(ks[12], (DEPTH, D_MODEL, MLP_HIDDEN), D_MODEL),
        "w_down": w(ks[13], (DEPTH, MLP_HIDDEN, D_MODEL), MLP_HIDDEN),
        "final_norm": gain(ks[14], (D_MODEL,)),
    }


def reference(x, positions, rel_bias, attn_norm, w_in, q_a_norm, w_q_b, kv_a_norm, w_kv_b,
              sinks, w_out, mlp_norm, w_up, w_down, final_norm):
    h = x
    cos, sin = rope_angles(positions)
    for l in range(DEPTH):
        a = rms_norm(h, attn_norm[l])
        proj = a @ w_in[l]
        c_q, c_kv, k_rope, q_s, k_s, v_s = jnp.split(proj, IN_SPLIT_POINTS, axis=-1)
        y_mla = mla_attention(c_q, c_kv, k_rope, cos, sin,
                              q_a_norm[l], w_q_b[l], kv_a_norm[l], w_kv_b[l])
        y_swa = swa_sink_attention(q_s, k_s, v_s, positions, rel_bias, sinks[l])
        h = h + jnp.concatenate([y_mla, y_swa], axis=-1) @ w_out[l]
        m = rms_norm(h, mlp_norm[l])
        h = h + jnp.square(jax.nn.relu(m @ w_up[l])) @ w_down[l]
    return rms_norm(h, final_norm)
```

```python
import math
import numpy as np
import ml_dtypes
import concourse.bass as bass
import concourse.mybir as mybir
from concourse.bass_utils import run_bass_kernel_spmd

F32 = mybir.dt.float32
BF16 = mybir.dt.bfloat16
I32 = mybir.dt.int32
AF = mybir.ActivationFunctionType
ALU = mybir.AluOpType

D = 2048
T = 1024
NT = 8
HID = 8192
EPS = 1e-6
NEG = -30000.0
MLA_SCALE = 192.0 ** -0.5
ENGS = ("tensor", "vector", "scalar", "gpsimd", "sync")
DEBUG = False


class Op:
    __slots__ = ("eng", "fn", "deps", "needed", "sem", "val", "is_dma", "inc", "order")

    def __init__(self, eng, fn, is_dma, sem):
        self.eng = eng
        self.fn = fn
        self.deps = []
        self.needed = False
        self.sem = sem
        self.val = None
        self.is_dma = is_dma
        self.inc = 16 if is_dma else 1


class Sched:
    def __init__(self, nc):
        self.nc = nc
        self.ops = []
        self.writers = {}
        self.readers = {}
        self.dma_last = {}
        self.bufs = {}
        self.live = {}
        self.ovl = {}

    def register(self, name, lo, hi):
        self.ovl[name] = []
        for o, (lo2, hi2) in self.bufs.items():
            if lo < hi2 and lo2 < hi:
                self.ovl[name].append(o)
                self.ovl[o].append(name)
        self.bufs[name] = (lo, hi)
        self.live[name] = set()

    def _bufname(self, region):
        return region[0] if isinstance(region, tuple) else region

    def _overlay_deps(self, region, deps):
        name = self._bufname(region)
        if name not in self.bufs:
            return
        for other in self.ovl[name]:
            if self.live[other]:
                for rk in self.live[other]:
                    deps.extend(self.writers.pop(rk, ()))
                    deps.extend(self.readers.pop(rk, ()))
                self.live[other] = set()
        self.live[name].add(region)

    def op(self, eng, fn, reads=(), writes=(), dma=None):
        is_dma = dma is not None
        o = Op(eng, fn, is_dma, ("dma", dma) if is_dma else ("eng", eng))
        o.order = len(self.ops)
        deps = []
        for r in list(reads) + list(writes):
            self._overlay_deps(r, deps)
        for r in reads:
            deps.extend(self.writers.get(r, ()))
        for w in writes:
            deps.extend(self.readers.get(w, ()))
            deps.extend(self.writers.get(w, ()))
        latest = {}
        for d in deps:
            if d.is_dma:
                d = self.dma_last[d.sem]
            if d is o:
                continue
            if d.eng == "tensor" and eng == "tensor" and not d.is_dma and not is_dma:
                continue
            cur = latest.get(d.sem)
            if cur is None or d.order > cur.order:
                latest[d.sem] = d
        for d in latest.values():
            o.deps.append(d)
            d.needed = True
        for r in reads:
            self.readers.setdefault(r, []).append(o)
        for w in writes:
            if self.readers.get(w):
                self.writers[w] = [o]
                self.readers[w] = []
            else:
                self.writers.setdefault(w, []).append(o)
        if is_dma:
            self.dma_last[o.sem] = o
            o.needed = True
        o.order = len(self.ops)
        self.ops.append(o)
        return o

    def emit(self, final=(), final_eng="sync"):
        nc = self.nc
        counters = {}
        for o in self.ops:
            if o.needed:
                counters[o.sem] = counters.get(o.sem, 0) + o.inc
                o.val = counters[o.sem]
        sems = {}
        for i, k in enumerate(sorted(counters.keys(), key=str)):
            sems[k] = nc.alloc_semaphore("s%d_%s" % (i, str(k[1]).replace(" ", "")))
        per_eng = {e: [o for o in self.ops if o.eng == e] for e in ENGS}
        self.stats = {e: len(v) for e, v in per_eng.items()}
        self.n_sems = len(sems)
        self.counters = counters

        def run(engname):
            def body(eng):
                known = {}
                for o in per_eng[engname]:
                    need = {}
                    for d in o.deps:
                        if need.get(d.sem, 0) < d.val:
                            need[d.sem] = d.val
                    for k, v in need.items():
                        if known.get(k, 0) < v:
                            eng.wait_ge(sems[k], v)
                            known[k] = v
                    ins = o.fn(eng)
                    if o.needed:
                        ins.then_inc(sems[o.sem], o.inc)
                if engname == final_eng:
                    for f in final:
                        k = ("dma", f)
                        eng.wait_ge(sems[k], counters[k])
            return body

        with nc.Block() as block:
            block.tensor(run("tensor"))
            block.vector(run("vector"))
            block.scalar(run("scalar"))
            block.gpsimd(run("gpsimd"))
            block.sync(run("sync"))


BASE = 17408
R0, R1, R2, R3, R4 = 0, 65536, 98304, 131072, 163840
R4P = 178432

CB_ID, CB_ID8, CB_ONES, CB_OP0, CB_OP1, CB_STAIR, CB_N = 0, 128, 256, 384, 512, 640, 640 + 2048
CF_INV, CF_SGN, CF_OTH, CF_ZERO, CF_GQ, CF_GKV, CF_SINK, CF_N = 0, 1, 2, 3, 4, 8, 12, 20


def build_program(stop=None):
    nc = bass.Bass("TRN2", target_bir_lowering=False)
    S = Sched(nc)

    class Lazy:
        def __init__(self, name, shape, dt=F32):
            self.name, self.shape, self.dt, self.t = name, shape, dt, None

        def get(self):
            if self.t is None:
                self.t = nc.dram_tensor(self.name, list(self.shape), self.dt, kind="ExternalInput")
                used_inputs.append(self.name)
            return self.t

        def ap(self):
            return self.get().ap()

    used_inputs = []
    x_own = Lazy("x_own", [T, D])
    x_oth = Lazy("x_oth", [T, D])
    pos_own = Lazy("pos_own", [T], I32)
    pos_oth = Lazy("pos_oth", [T], I32)
    cbd = Lazy("cb", [128, CB_N], BF16)
    cfd = Lazy("cf", [128, CF_N])
    swabd = Lazy("swab", [128, 4096])
    g_attn = Lazy("g_attn", [D])
    g_mlp = Lazy("g_mlp", [D])
    g_fin = Lazy("g_fin", [D])
    w_in = Lazy("w_in", [D, 2368])
    w_qb = Lazy("w_qb", [512, 1536])
    w_kvb = Lazy("w_kvb", [512, 2048])
    w_out = Lazy("w_out", [D, D])
    w_up = Lazy("w_up", [D, HID])
    w_dn = Lazy("w_dn", [HID, D])
    out_d = nc.dram_tensor("out", [T, D], F32, kind="ExternalOutput")
    dbg = {}

    def finish():
        import os
        trunc = int(os.environ.get("KTRUNC", "0"))
        if trunc:
            S.ops = S.ops[:trunc]
            for o in S.ops:
                o.needed = o.is_dma
            for o in S.ops:
                for d in o.deps:
                    d.needed = True
            S.emit(final=[])
            S.used_inputs = used_inputs
            return nc, S
        finals = ["dbg_" + k for k in dbg]
        if stop is None:
            finals += ["ostage0", "ostage1"]
        S.emit(final=finals)
        S.used_inputs = used_inputs
        return nc, S

    def sb(name, shape, dt, off):
        nbytes = int(np.prod(shape[1:])) * (2 if dt == BF16 else 4)
        assert off + nbytes <= 211968, (name, off, nbytes)
        t = nc.alloc_sbuf_tensor_at(name, list(shape), dt, offset=BASE + off)
        S.register(name, off, off + nbytes)
        return t

    cqn = sb("cqn", [128, 4, T], BF16, R0 + 0)
    ckvn = sb("ckvn", [128, 4, 2 * T], BF16, R0 + 8192)
    kr = sb("kr", [128, 2 * T], BF16, R0 + 24576)
    q_s = sb("q_s", [128, 8, T], BF16, R0 + 28672)
    k_s = sb("k_s", [128, 9 * 128], BF16, R0 + 45056)
    vpad = sb("vpad", [128, 9, 2, 2, 128], BF16, R0 + 47360)
    cs_own = sb("cs_own", [128, 2, T], F32, R0 + 56576)
    h = sb("h", [128, NT, D], F32, R0)
    aT = sb("aT", [128, 16, T], BF16, R1)
    yT = sb("yT", [128, 16, T], BF16, R1)
    mT = sb("mT", [128, 16, T], BF16, R1)
    win = [sb("win%d" % i, [128, 16, 512], BF16, R2 + 16384 * i) for i in range(2)]
    qn = [sb("qn%d" % i, [128, T], BF16, R2 + 2048 * i) for i in range(2)]
    qr = [sb("qr%d" % i, [128, T], BF16, R2 + 4096 + 2048 * i) for i in range(2)]
    kn = [sb("kn%d" % i, [128, 2 * T], BF16, R2 + 8192 + 4096 * i) for i in range(2)]
    vh = [sb("vh%d" % i, [128, 16, 128], BF16, R2 + 16384 + 4096 * i) for i in range(2)]
    qraw = sb("qraw", [128, 512], F32, R2 + 24576)
    qswp = sb("qswp", [128, 512], F32, R2 + 26624)
    qt1 = sb("qt1", [128, 512], F32, R2 + 28672)
    qt2 = sb("qt2", [128, 512], F32, R2 + 30720)
    wout = [sb("wout%d" % i, [128, 16, 512], BF16, R2 + 16384 * i) for i in range(2)]
    wup = [sb("wup%d" % i, [128, 16, 512], BF16, R2 + 16384 * i) for i in range(2)]
    ostage = [sb("ostage%d" % i, [128, D], F32, R2 + 8192 * i) for i in range(2)]
    xstage = [sb("xstage%d" % i, [128, D], F32, R3 + 8192 * i) for i in range(2)]
    krraw = sb("krraw", [128, 512], F32, R3 + 16384)
    krswp = sb("krswp", [128, 512], F32, R3 + 18432)
    rt1 = sb("rt1", [128, 512], F32, R3 + 20480)
    rt2 = sb("rt2", [128, 512], F32, R3 + 22528)
    ropei = sb("ropei", [128, 512], I32, R3 + 24576)
    ropef = sb("ropef", [128, 2, 512], F32, R3 + 26624)
    wkvb = sb("wkvb", [128, 4, 2048], BF16, R3)
    wqb = sb("wqb", [128, 4, 1536], BF16, R3 + 16384)
    wdn = [sb("wdn%d" % i, [128, 4, D], BF16, R3 + 16384 * i) for i in range(2)]
    gbc = sb("gbc", [128, D], F32, R4)
    cb = sb("cb", [128, CB_N], BF16, R4 + 8192)
    cf = sb("cf", [128, CF_N], F32, R4 + 13568)
    ss = sb("ss", [128, 32], F32, R4 + 13568 + 128)
    ms = sb("ms", [128, 32], F32, R4 + 13568 + 256)
    rstd = sb("rstd", [128, 32], F32, R4 + 13568 + 384)
    esink = sb("esink", [128, 8], F32, R4 + 13568 + 512)
    xn = [sb("xn%d" % i, [128, D], BF16, R4P + 4096 * i) for i in range(2)]
    sq = sb("sq", [128, D], BF16, R4P + 8192)
    cf32 = sb("cf32", [128, 4, 512], F32, R4P + 12288)
    rstdbc = sb("rstdbc", [128, 512], F32, R4P + 20480)
    lntmp = sb("lntmp", [128, 512], F32, R4P + 22528)
    cs_oth = sb("cs_oth", [128, 2, T], F32, R4P + 24576)
    pt = [sb("pt%d" % i, [128, 512], BF16, R4P + 1024 * i) for i in range(3)]
    swab = sb("swab", [128, 16, 2, 128], BF16, R4P + 3072)
    kpad = sb("kpad", [128, 2, 2, 9 * 128], BF16, R4P + 11264)
    rec = [sb("rec%d" % i, [128, 512], F32, R4P + 20480 + 2048 * i) for i in range(2)]
    dtmp = sb("dtmp", [128, 512], F32, R4P + 24576)
    sinkbc = sb("sinkbc", [128, 8, 128], F32, R4P + 26624)
    hid = [sb("hid%d" % i, [128, 4, T], BF16, R4P + 12288 + 8192 * i) for i in range(2)]
    r32 = [sb("r32_%d" % i, [128, 512], F32, R4P + 28672 + 2048 * i) for i in range(2)]

    ps = [nc.alloc_psum_tensor("ps%d" % i, [128, 512], F32) for i in range(8)]
    psn = ["ps%d" % i for i in range(8)]

    ident = cb[:, CB_ID:CB_ID + 128]
    ident8 = cb[:, CB_ID8:CB_ID8 + 128]
    ones = cb[:, CB_ONES:CB_ONES + 128]
    onespad = [cb[:, CB_OP0:CB_OP0 + 128], cb[:, CB_OP1:CB_OP1 + 128]]

    def stair(j):
        return cb[:, CB_STAIR + 512 * j: CB_STAIR + 512 * (j + 1)]

    def dump(name, ap, shape, dt, regions):
        if not DEBUG and stop is None:
            return
        t = nc.dram_tensor("dbg_" + name, list(shape), dt, kind="ExternalOutput")
        dbg[name] = t
        S.op("sync", lambda e: e.dma_start(out=t.ap(), in_=ap), reads=regions, dma="dbg_" + name)

    def mm(out_ap, lhsT, rhs, start, stop, reads, writes):
        S.op("tensor", lambda e: e.matmul(out_ap, lhsT=lhsT, rhs=rhs, start=start, stop=stop),
             reads=reads, writes=writes)

    S.op("sync", lambda e: e.dma_start(out=cb[:], in_=cbd.ap()), writes=["cb"], dma="const")
    S.op("sync", lambda e: e.dma_start(out=cf[:], in_=cfd.ap()), writes=["cf"], dma="const")
    S.op("sync", lambda e: e.dma_start(out=gbc[:], in_=bass.AP(g_attn.get(), 0, [[0, 128], [1, D]])),
         writes=["gbc"], dma="gbc")
    S.op("vector", lambda e: e.memset(ss[:], 0.0), writes=["ss"])
    S.op("vector", lambda e: e.memset(vpad[:], 0.0), writes=[("vpad", 0), ("vpad", 1), ("vpad", 5)])

    def wload(dst, src_ap, region, group):
        S.op("gpsimd", lambda e: e.dma_start(out=dst, in_=src_ap), writes=[region], dma=group)

    def load_win(slot, c0, ncols, dst_c0=0, first=True):
        src = w_in.ap()[:, c0:c0 + ncols].rearrange("(k p) n -> p k n", p=128)
        wload(win[slot][:, :, dst_c0:dst_c0 + ncols], src, "win%d" % slot, "win%d" % slot)

    load_win(0, 512, 512)
    load_win(1, 1024, 64, 0)
    load_win(1, 2112, 256, 64)

    TWO_PI = 2.0 * math.pi

    def rope_tables(cs, pos_t, nm):
        for c in range(2):
            sl = slice(c * 512, (c + 1) * 512)
            S.op("sync", lambda e, c=c: e.dma_start(out=ropei[0:64, :],
                                                    in_=bass.AP(pos_t.get(), c * 512, [[0, 64], [1, 512]])),
                 writes=["ropei"], dma="ropei")
            S.op("vector", lambda e: e.tensor_copy(out=ropef[0:64, 0, :], in_=ropei[0:64, :]),
                 reads=["ropei"], writes=[("ropef", 0)])
            S.op("vector", lambda e: e.tensor_scalar(out=ropef[0:64, 0, :], in0=ropef[0:64, 0, :],
                                                     scalar1=cf[0:64, CF_INV:CF_INV + 1], scalar2=None,
                                                     op0=ALU.mult),
                 reads=[("ropef", 0), "cf"], writes=[("ropef", 0)])
            for which in range(2):
                dst = cs[0:64, which, sl]
                if which == 0:
                    S.op("vector", lambda e: e.tensor_scalar(out=ropef[0:64, 1, :], in0=ropef[0:64, 0, :],
                                                             scalar1=0.25, scalar2=None, op0=ALU.add),
                         reads=[("ropef", 0)], writes=[("ropef", 1)])
                else:
                    S.op("vector", lambda e: e.tensor_copy(out=ropef[0:64, 1, :], in_=ropef[0:64, 0, :]),
                         reads=[("ropef", 0)], writes=[("ropef", 1)])
                S.op("vector", lambda e: e.tensor_copy(out=ropei[0:64, :], in_=ropef[0:64, 1, :]),
                     reads=[("ropef", 1)], writes=["ropei"])
                S.op("vector", lambda e, dst=dst: e.tensor_copy(out=dst, in_=ropei[0:64, :]),
                     reads=["ropei"], writes=[(nm, c)])
                S.op("vector", lambda e, dst=dst: e.tensor_tensor(out=ropef[0:64, 1, :], in0=ropef[0:64, 1, :],
                                                                  in1=dst, op=ALU.subtract),
                     reads=[("ropef", 1), (nm, c)], writes=[("ropef", 1)])
                S.op("vector", lambda e, dst=dst: e.tensor_scalar(out=dst, in0=ropef[0:64, 1, :], scalar1=0.5,
                                                                  scalar2=None, op0=ALU.is_gt),
                     reads=[("ropef", 1)], writes=[(nm, c)])
                S.op("vector", lambda e, dst=dst: e.tensor_tensor(out=ropef[0:64, 1, :], in0=ropef[0:64, 1, :],
                                                                  in1=dst, op=ALU.subtract),
                     reads=[("ropef", 1), (nm, c)], writes=[("ropef", 1)])
                S.op("scalar", lambda e, dst=dst: e.activation(out=dst, in_=ropef[0:64, 1, :], func=AF.Sin,
                                                               scale=TWO_PI),
                     reads=[("ropef", 1)], writes=[(nm, c)])
                if which == 1:
                    S.op("vector", lambda e, dst=dst: e.tensor_scalar(out=dst, in0=dst,
                                                                      scalar1=cf[0:64, CF_SGN:CF_SGN + 1],
                                                                      scalar2=None, op0=ALU.mult),
                         reads=[(nm, c), "cf"], writes=[(nm, c)])

    rope_tables(cs_oth, pos_oth, "cs_oth")
    rope_tables(cs_own, pos_own, "cs_own")
    if stop == 10:
        dump("cs_own", cs_own[0:64, :, :], [64, 2, T], F32, [("cs_own", 0), ("cs_own", 1)])
        return finish()

    state = {"col": 0, "projbank": 0}

    def norm_stats(src_ap, src_regions, col):
        S.op("scalar", lambda e: e.activation(out=sq[:], in_=src_ap, func=AF.Square,
                                              accum_out=ss[:, col:col + 1]),
             reads=list(src_regions) + ["ss"], writes=["sq", ("ssc", col)])
        S.op("vector", lambda e: e.tensor_scalar(out=ms[:, col:col + 1], in0=ss[:, col:col + 1],
                                                 scalar1=1.0 / D, scalar2=EPS, op0=ALU.mult, op1=ALU.add),
             reads=[("ssc", col)], writes=[("ms", col)])
        S.op("scalar", lambda e: e.activation(out=ms[:, col:col + 1], in_=ms[:, col:col + 1], func=AF.Ln),
             reads=[("ms", col)], writes=[("ms", col)])
        S.op("scalar", lambda e: e.activation(out=rstd[:, col:col + 1], in_=ms[:, col:col + 1], func=AF.Exp,
                                              scale=-0.5),
             reads=[("ms", col)], writes=[("rstd", col)])

    def norm_apply_T(src_ap, src_regions, col, xb, dstT, dst_region, tok0):
        S.op("vector", lambda e: e.scalar_tensor_tensor(out=xn[xb][:], in0=src_ap, scalar=rstd[:, col:col + 1],
                                                        in1=gbc[:], op0=ALU.mult, op1=ALU.mult),
             reads=list(src_regions) + [("rstd", col), "gbc"], writes=["xn%d" % xb])
        for k in range(16):
            bank = 6 + k // 8
            pv = ps[bank][:].bitcast(BF16)
            S.op("tensor", lambda e, k=k, pv=pv: e.transpose(out=pv[:, (k % 8) * 128:(k % 8 + 1) * 128],
                                                             in_=xn[xb][:, k * 128:(k + 1) * 128],
                                                             identity=ident),
                 reads=["xn%d" % xb, "cb"], writes=[psn[bank]])

    def norm_evac(dstT, dst_region, tok0):
        for half, eng in ((0, "scalar"), (1, "vector")):
            bank = 6 + half
            pv = ps[bank][:].bitcast(BF16).rearrange("p (k t) -> p k t", k=8)
            dst = dstT[:, half * 8:(half + 1) * 8, tok0:tok0 + 128]
            if eng == "scalar":
                S.op("scalar", lambda e, dst=dst, pv=pv: e.copy(out=dst, in_=pv), reads=[psn[bank]],
                     writes=[dst_region])
            else:
                S.op("vector", lambda e, dst=dst, pv=pv: e.tensor_copy(out=dst, in_=pv), reads=[psn[bank]],
                     writes=[dst_region])

    def next_bank():
        b = state["projbank"] % 4
        state["projbank"] += 1
        return b

    def latent_chunk(piece, a_tok0, gcol, dst, dst_tok0, dst_name):
        import os
        if os.environ.get("KPRINT"):
            print("latent_chunk starts at op", len(S.ops), [(i, o.eng) for i, o in enumerate(S.ops[-3:])])
        sqv = sq[:].rearrange("p (m t) -> p m t", m=4)
        a_regs = [("aT", (a_tok0 // 128) + i) for i in range(4)]
        for m in range(4):
            b = next_bank()
            for k in range(16):
                mm(ps[b][:], win[piece][:, k, m * 128:(m + 1) * 128], aT[:, k, a_tok0:a_tok0 + 512],
                   k == 0, k == 15, a_regs + ["win%d" % piece], [psn[b]])
            S.op("vector", lambda e, b=b, m=m: e.tensor_copy(out=cf32[:, m, :], in_=ps[b][:]),
                 reads=[psn[b]], writes=[("cf32", m)])
            S.op("scalar", lambda e, b=b, m=m: e.activation(out=sqv[:, m, :], in_=cf32[:, m, :], func=AF.Square),
                 reads=[("cf32", m)], writes=["sq"])
        for m in range(4):
            mm(ps[4][:], ones, sqv[:, m, :], m == 0, m == 3, ["sq", "cb"], [psn[4]])
        S.op("vector", lambda e: e.tensor_scalar(out=lntmp[:], in0=ps[4][:], scalar1=1.0 / 512, scalar2=EPS,
                                                 op0=ALU.mult, op1=ALU.add),
             reads=[psn[4]], writes=["lntmp"])
        S.op("scalar", lambda e: e.activation(out=lntmp[:], in_=lntmp[:], func=AF.Ln),
             reads=["lntmp"], writes=["lntmp"])
        S.op("scalar", lambda e: e.activation(out=rstdbc[:], in_=lntmp[:], func=AF.Exp, scale=-0.5),
             reads=["lntmp"], writes=["rstdbc"])
        for m in range(4):
            S.op("vector", lambda e, m=m: e.scalar_tensor_tensor(
                out=dst[:, m, dst_tok0:dst_tok0 + 512], in0=cf32[:, m, :],
                scalar=cf[:, gcol + m:gcol + m + 1], in1=rstdbc[:], op0=ALU.mult, op1=ALU.mult),
                 reads=[("cf32", m), "rstdbc", "cf"], writes=[(dst_name, m, dst_tok0 // 512)])

    def kv_misc_chunk(a_tok0, ctx_tok0, is_own, cs, cs_name):
        a_regs = [("aT", (a_tok0 // 128) + i) for i in range(4)]
        cc = ctx_tok0 // 512
        b = next_bank()
        for k in range(16):
            mm(ps[b][0:64, :], win[1][:, k, 0:64], aT[:, k, a_tok0:a_tok0 + 512], k == 0, k == 15,
               a_regs + ["win1"], [psn[b]])
        S.op("vector", lambda e, b=b: e.tensor_copy(out=krraw[0:64, :], in_=ps[b][0:64, :]),
             reads=[psn[b]], writes=["krraw"])
        S.op("sync", lambda e: e.dma_start(out=krswp[0:32, :], in_=krraw[32:64, :]), reads=["krraw"],
             writes=["krswp"], dma="krswp")
        S.op("sync", lambda e: e.dma_start(out=krswp[32:64, :], in_=krraw[0:32, :]), reads=["krraw"],
             writes=["krswp"], dma="krswp")
        csl = slice((ctx_tok0 % T), (ctx_tok0 % T) + 512)
        cchunk = (ctx_tok0 % T) // 512
        S.op("vector", lambda e: e.tensor_tensor(out=rt1[0:64, :], in0=krraw[0:64, :], in1=cs[0:64, 0, csl],
                                                 op=ALU.mult),
             reads=["krraw", (cs_name, cchunk)], writes=["rt1"])
        S.op("vector", lambda e: e.tensor_tensor(out=rt2[0:64, :], in0=krswp[0:64, :], in1=cs[0:64, 1, csl],
                                                 op=ALU.mult),
             reads=["krswp", (cs_name, cchunk)], writes=["rt2"])
        S.op("vector", lambda e: e.tensor_tensor(out=kr[0:64, ctx_tok0:ctx_tok0 + 512], in0=rt1[0:64, :],
                                                 in1=rt2[0:64, :], op=ALU.add),
             reads=["rt1", "rt2"], writes=[("kr", cc)])
        if is_own:
            b = next_bank()
            for k in range(16):
                mm(ps[b][:], win[1][:, k, 64:192], aT[:, k, a_tok0:a_tok0 + 512], k == 0, k == 15,
                   a_regs + ["win1"], [psn[b]])
            s0 = 128 + a_tok0
            S.op("scalar", lambda e, b=b: e.copy(out=k_s[:, s0:s0 + 512], in_=ps[b][:]), reads=[psn[b]],
                 writes=[("k_s", 1 + a_tok0 // 512)])
            tiles = [(a_tok0 + 128 * i, 1 + a_tok0 // 128 + i) for i in range(4)]
        elif a_tok0 == 512:
            b = next_bank()
            for k in range(16):
                mm(ps[b][:, 0:128], win[1][:, k, 64:192], aT[:, k, 896:1024], k == 0, k == 15,
                   [("aT", 7), "win1"], [psn[b]])
            S.op("scalar", lambda e, b=b: e.copy(out=k_s[:, 0:128], in_=ps[b][:, 0:128]), reads=[psn[b]],
                 writes=[("k_s", 0)])
            tiles = [(896, 0)]
        else:
            tiles = []
        if tiles:
            b = next_bank()
            for i, (t0, slot) in enumerate(tiles):
                for k in range(16):
                    mm(ps[b][:, i * 128:(i + 1) * 128], aT[:, k, t0:t0 + 128], win[1][:, k, 192:320],
                       k == 0, k == 15, [("aT", t0 // 128), "win1"], [psn[b]])
            n = len(tiles)
            slot0 = tiles[0][1]
            psv = ps[b][:].rearrange("p (i c) -> p i c", i=4)
            for g in range(2):
                for j in range(2):
                    S.op("vector", lambda e, g=g, j=j, psv=psv: e.tensor_copy(
                        out=vpad[:, slot0:slot0 + n, g, j, j * 64:(j + 1) * 64],
                        in_=psv[:, 0:n, g * 64:(g + 1) * 64]),
                         reads=[psn[b]], writes=[("vpad", slot0)])

    def q_swa_chunk(piece, tbase, a_tok0):
        a_regs = [("aT", (a_tok0 // 128) + i) for i in range(4)]
        for tl in range(4):
            t = tbase + tl
            b = next_bank()
            for k in range(16):
                mm(ps[b][:], win[piece][:, k, tl * 128:(tl + 1) * 128], aT[:, k, a_tok0:a_tok0 + 512],
                   k == 0, k == 15, a_regs + ["win%d" % piece], [psn[b]])
            if tl % 2 == 0:
                S.op("scalar", lambda e, b=b, t=t: e.copy(out=q_s[:, t, a_tok0:a_tok0 + 512], in_=ps[b][:]),
                     reads=[psn[b]], writes=[("q_s", t, a_tok0 // 512)])
            else:
                S.op("vector", lambda e, b=b, t=t: e.tensor_copy(out=q_s[:, t, a_tok0:a_tok0 + 512],
                                                                 in_=ps[b][:]),
                     reads=[psn[b]], writes=[("q_s", t, a_tok0 // 512)])

    def norm_pass(src_tiles, dstT, dst_name, after_tile):
        n = len(src_tiles)
        cols = []
        for i in range(n + 2):
            if i >= 2:
                j = i - 2
                norm_evac(dstT, (dst_name, j), j * 128)
                after_tile(j)
            if 1 <= i <= n:
                j = i - 1
                ap, regs, _ = src_tiles[j]
                norm_apply_T(ap, regs, cols[j], j % 2, dstT, (dst_name, j), j * 128)
            if i < n:
                ap, regs, loader = src_tiles[i]
                if loader is not None:
                    loader()
                col = state["col"]
                state["col"] += 1
                cols.append(col)
                norm_stats(ap, regs, col)

    def x_tiles(xd, nm):
        tiles = []
        for i in range(NT):
            slot = state.setdefault("xslot", 0)
            state["xslot"] += 1
            sl = slot % 2

            def loader(i=i, sl=sl):
                S.op("sync", lambda e: e.dma_start(out=xstage[sl][:], in_=xd.ap()[i * 128:(i + 1) * 128, :]),
                     writes=["xstage%d" % sl], dma="xstage%d" % sl)
            tiles.append((xstage[sl][:], ["xstage%d" % sl], loader))
        return tiles

    def after_oth(j):
        if j % 4 == 3:
            c = j // 4
            latent_chunk(0, c * 512, CF_GKV, ckvn, T + c * 512, "ckvn")
            if stop != 13:
                kv_misc_chunk(c * 512, T + c * 512, False, cs_oth, "cs_oth")

    norm_pass(x_tiles(x_oth, "oth"), aT, "aT", (lambda j: None) if stop == 11 else after_oth)
    if stop in (11, 12, 13):
        dump("aT", aT[:], [128, 16, T], BF16, [("aT", j) for j in range(8)])
        if stop in (12, 13):
            dump("ckvn", ckvn[:], [128, 4, 2 * T], BF16, [("ckvn", m, c) for m in range(4) for c in (2, 3)])
            if stop == 12:
                dump("kr", kr[0:64, :], [64, 2 * T], BF16, [("kr", c) for c in (2, 3)])
        return finish()

    def after_own(j):
        if j % 4 == 3:
            c = j // 4
            latent_chunk(0, c * 512, CF_GKV, ckvn, c * 512, "ckvn")
            kv_misc_chunk(c * 512, c * 512, True, cs_own, "cs_own")

    norm_pass(x_tiles(x_own, "own"), aT, "aT", after_own)

    load_win(0, 0, 512)
    S.op("gpsimd", lambda e: e.dma_start(out=wkvb[:], in_=w_kvb.ap().rearrange("(k p) n -> p k n", p=128)),
         writes=["wkvb"], dma="wkvb")
    for c in range(2):
        latent_chunk(0, c * 512, CF_GQ, cqn, c * 512, "cqn")
    load_win(1, 1088, 512)
    S.op("gpsimd", lambda e: e.dma_start(out=wqb[:], in_=w_qb.ap().rearrange("(k p) n -> p k n", p=128)),
         writes=["wqb"], dma="wqb")
    for c in range(2):
        q_swa_chunk(1, 0, c * 512)
    load_win(0, 1600, 512)
    S.op("gpsimd", lambda e: e.dma_start(out=swab[:].rearrange("p h k q -> p (h k q)"), in_=swabd.ap()),
         writes=["swab"], dma="swab")
    for c in range(2):
        q_swa_chunk(0, 4, c * 512)

    S.op("vector", lambda e: e.memset(kpad[:], 0.0), writes=["kpad"])
    for g in range(2):
        for hf in range(2):
            S.op("sync", lambda e, g=g, hf=hf: e.dma_start(out=kpad[hf * 64:(hf + 1) * 64, g, hf, :],
                                                           in_=k_s[g * 64:(g + 1) * 64, :]),
                 reads=[("k_s", i) for i in range(3)], writes=["kpad"], dma="kpad")

    dump("ckvn", ckvn[:], [128, 4, 2 * T], BF16, [("ckvn", m, c) for m in range(4) for c in range(4)])
    dump("cqn", cqn[:], [128, 4, T], BF16, [("cqn", m, c) for m in range(4) for c in range(2)])
    dump("kr", kr[0:64, :], [64, 2 * T], BF16, [("kr", c) for c in range(4)])
    dump("q_s", q_s[:], [128, 8, T], BF16, [("q_s", t, c) for t in range(8) for c in range(2)])
    dump("k_s", k_s[:], [128, 9 * 128], BF16, [("k_s", i) for i in range(3)])
    if stop == 1:
        return finish()

    jobbank = {"i": 0}

    def jb():
        b = jobbank["i"] % 2
        jobbank["i"] += 1
        return b

    def head_jobs(hd):
        hb = hd % 2
        jobs = []
        cq_regs = lambda c: [("cqn", m, c) for m in range(4)]
        ckv_regs = lambda cc: [("ckvn", m, cc) for m in range(4)]

        def job_qn(c):
            b = jb()
            for k in range(4):
                mm(ps[b][:], wqb[:, k, hd * 192:hd * 192 + 128], cqn[:, k, c * 512:(c + 1) * 512],
                   k == 0, k == 3, cq_regs(c) + ["wqb"], [psn[b]])
            S.op("vector", lambda e: e.tensor_copy(out=qn[hb][:, c * 512:(c + 1) * 512], in_=ps[b][:]),
                 reads=[psn[b]], writes=[("qn%d" % hb, c)])

        def job_qr(c):
            b = jb()
            for k in range(4):
                mm(ps[b][0:64, :], wqb[:, k, hd * 192 + 128:hd * 192 + 192], cqn[:, k, c * 512:(c + 1) * 512],
                   k == 0, k == 3, cq_regs(c) + ["wqb"], [psn[b]])
            S.op("vector", lambda e: e.tensor_copy(out=qraw[0:64, :], in_=ps[b][0:64, :]),
                 reads=[psn[b]], writes=["qraw"])
            S.op("sync", lambda e: e.dma_start(out=qswp[0:32, :], in_=qraw[32:64, :]), reads=["qraw"],
                 writes=["qswp"], dma="qswp")
            S.op("sync", lambda e: e.dma_start(out=qswp[32:64, :], in_=qraw[0:32, :]), reads=["qraw"],
                 writes=["qswp"], dma="qswp")
            csl = slice(c * 512, (c + 1) * 512)
            S.op("vector", lambda e: e.tensor_tensor(out=qt1[0:64, :], in0=qraw[0:64, :],
                                                     in1=cs_own[0:64, 0, csl], op=ALU.mult),
                 reads=["qraw", ("cs_own", c)], writes=["qt1"])
            S.op("vector", lambda e: e.tensor_tensor(out=qt2[0:64, :], in0=qswp[0:64, :],
                                                     in1=cs_own[0:64, 1, csl], op=ALU.mult),
                 reads=["qswp", ("cs_own", c)], writes=["qt2"])
            S.op("vector", lambda e: e.tensor_tensor(out=qr[hb][0:64, csl], in0=qt1[0:64, :], in1=qt2[0:64, :],
                                                     op=ALU.add),
                 reads=["qt1", "qt2"], writes=[("qr%d" % hb, c)])

        def job_kn(cc):
            b = jb()
            for k in range(4):
                mm(ps[b][:], wkvb[:, k, hd * 256:hd * 256 + 128], ckvn[:, k, cc * 512:(cc + 1) * 512],
                   k == 0, k == 3, ckv_regs(cc) + ["wkvb"], [psn[b]])
            S.op("vector", lambda e: e.tensor_copy(out=kn[hb][:, cc * 512:(cc + 1) * 512], in_=ps[b][:]),
                 reads=[psn[b]], writes=[("kn%d" % hb, cc)])

        def job_v(cc):
            b = jb()
            for i in range(4):
                t0 = cc * 512 + i * 128
                for k in range(4):
                    mm(ps[b][:, i * 128:(i + 1) * 128], ckvn[:, k, t0:t0 + 128],
                       wkvb[:, k, hd * 256 + 128:hd * 256 + 256], k == 0, k == 3,
                       ckv_regs(cc) + ["wkvb"], [psn[b]])
            S.op("vector", lambda e: e.tensor_copy(out=vh[hb][:, cc * 4:(cc + 1) * 4, :],
                                                   in_=ps[b][:].rearrange("p (i c) -> p i c", i=4)),
                 reads=[psn[b]], writes=[("vh%d" % hb, cc)])

        for c in range(2):
            jobs.append(lambda c=c: job_qn(c))
        for c in range(2):
            jobs.append(lambda c=c: job_qr(c))
        for cc in range(4):
            jobs.append(lambda cc=cc: job_kn(cc))
        for cc in range(4):
            jobs.append(lambda cc=cc: job_v(cc))
        return jobs

    units = []
    for hd in range(8):
        for c in range(2):
            ul = [("oth", 8 + bb, None) for bb in range(8)]
            ul += [("own", bb, None) for bb in range(4 * c)]
            ul += [("stair", 4 * c + j, j) for j in range(4)]
            for i, (kind, slot, j) in enumerate(ul):
                units.append(dict(hd=hd, c=c, kind=kind, slot=slot, j=j, first=(i == 0), last=(i == len(ul) - 1),
                                  grp=hd * 2 + c))
    for i, u in enumerate(units):
        u["idx"] = i

    def unit_scores(u):
        hb = u["hd"] % 2
        b = 2 + u["idx"] % 2
        c = u["c"]
        slot = u["slot"]
        ksl = slice(slot * 128, (slot + 1) * 128)
        qsl = slice(c * 512, (c + 1) * 512)
        st = u["kind"] == "stair"
        mm(ps[b][:], kn[hb][:, ksl], qn[hb][:, qsl], True, False,
           [("kn%d" % hb, slot // 4), ("qn%d" % hb, c)], [psn[b]])
        mm(ps[b][:], kr[0:64, ksl], qr[hb][0:64, qsl], False, not st,
           [("kr", slot // 4), ("qr%d" % hb, c)], [psn[b]])
        if st:
            mm(ps[b][:], ident, stair(u["j"]), False, True, ["cb"], [psn[b]])
        pb = u["idx"] % 3
        bcol = CF_OTH if u["kind"] == "oth" else CF_ZERO
        S.op("scalar", lambda e: e.activation(out=pt[pb][:], in_=ps[b][:], func=AF.Exp,
                                              bias=cf[:, bcol:bcol + 1], scale=MLA_SCALE),
             reads=[psn[b], "cf"], writes=["pt%d" % pb])

    def unit_pv(u):
        hb = u["hd"] % 2
        pb = u["idx"] % 3
        st = 4 + 2 * (u["grp"] % 2)
        slot = u["slot"]
        mm(ps[st][:], vh[hb][:, slot, :], pt[pb][:], u["first"], u["last"],
           [("vh%d" % hb, slot // 4), "pt%d" % pb], [psn[st]])
        mm(ps[st + 1][:], ones, pt[pb][:], u["first"], u["last"], ["cb", "pt%d" % pb], [psn[st + 1]])
        if u["last"]:
            rb = u["grp"] % 2
            hd, c = u["hd"], u["c"]
            S.op("vector", lambda e: e.reciprocal(out=rec[rb][:], in_=ps[st + 1][:]), reads=[psn[st + 1]],
                 writes=["rec%d" % rb])
            S.op("vector", lambda e: e.tensor_tensor(out=yT[:, hd, c * 512:(c + 1) * 512], in0=ps[st][:],
                                                     in1=rec[rb][:], op=ALU.mult),
                 reads=[psn[st], "rec%d" % rb], writes=[("yT", hd, 4 * c + i) for i in range(4)])

    import os
    KP = os.environ.get("KPRINT")
    if KP:
        print("phase2 start", len(S.ops))
    for jfn in head_jobs(0):
        jfn()
        if KP:
            print(" job done", len(S.ops))
    pending = []
    nxt_jobs = []
    for i, u in enumerate(units):
        if u["first"] and u["c"] == 0:
            nxt_jobs = head_jobs(u["hd"] + 1) if u["hd"] < 7 else []
        if i == 0:
            unit_scores(units[0])
            unit_scores(units[1])
        unit_pv(u)
        if i + 2 < len(units):
            unit_scores(units[i + 2])
        if KP and (i < 14 or i % 28 == 0):
            print(" unit", i, u["kind"], "done", len(S.ops))
        local = i - (u["hd"] * 28)
        if nxt_jobs and local % 2 == 1 and local < 26:
            nxt_jobs.pop(0)()
    assert not nxt_jobs

    S.op("scalar", lambda e: e.activation(out=esink[:], in_=cf[:, CF_SINK:CF_SINK + 8], func=AF.Exp),
         reads=["cf"], writes=["esink"])
    S.op("vector", lambda e: e.tensor_copy(out=sinkbc[:], in_=esink[:].unsqueeze(2).to_broadcast([128, 8, 128])),
         reads=["esink"], writes=["sinkbc"])
    swa_units = [(n, t) for n in range(8) for t in range(8)]
    u0 = len(units)

    def swa_scores(idx, n, t):
        b = 2 + (u0 + idx) % 2
        g = t // 4
        mm(ps[b][:], ident8, swab[:, 2 * t:2 * t + 2, :, :].rearrange("p h k q -> p (h k q)"), True, False,
           ["cb", "swab"], [psn[b]])
        for j in range(2):
            for kb in range(2):
                slot = n + kb
                col = (j * 2 + kb) * 128
                mm(ps[b][:, col:col + 128], kpad[:, g, j, slot * 128:(slot + 1) * 128],
                   q_s[:, t, n * 128:(n + 1) * 128], False, (j == 1 and kb == 1),
                   ["kpad", ("q_s", t, n // 4)], [psn[b]])
        pb = (u0 + idx) % 3
        if n == 0:
            pv3 = pt[pb][:].rearrange("p (j k q) -> p j k q", j=2, k=2)
            sv3 = ps[b][:].rearrange("p (j k q) -> p j k q", j=2, k=2)
            S.op("scalar", lambda e: e.activation(out=pv3[:, :, 0, :], in_=sv3[:, :, 0, :], func=AF.Exp,
                                                  bias=cf[:, CF_OTH:CF_OTH + 1], scale=0.125),
                 reads=[psn[b], "cf"], writes=["pt%d" % pb])
            S.op("scalar", lambda e: e.activation(out=pv3[:, :, 1, :], in_=sv3[:, :, 1, :], func=AF.Exp,
                                                  bias=cf[:, CF_ZERO:CF_ZERO + 1], scale=0.125),
                 reads=[psn[b], "cf"], writes=["pt%d" % pb])
        else:
            S.op("scalar", lambda e: e.activation(out=pt[pb][:], in_=ps[b][:], func=AF.Exp,
                                                  bias=cf[:, CF_ZERO:CF_ZERO + 1], scale=0.125),
                 reads=[psn[b], "cf"], writes=["pt%d" % pb])

    def swa_pv(idx, n, t):
        pb = (u0 + idx) % 3
        grp = n * 2 + t // 4
        st = 4 + 2 * (grp % 2)
        g = t // 4
        tl = t % 4
        for which in range(2):
            for j in range(2):
                for kb in range(2):
                    slot = n + kb
                    col = (j * 2 + kb) * 128
                    lhs = vpad[:, slot, g, j, :] if which == 0 else onespad[j]
                    first = (j == 0 and kb == 0)
                    last = (j == 1 and kb == 1)
                    mm(ps[st + which][:, tl * 128:(tl + 1) * 128], lhs, pt[pb][:, col:col + 128], first, last,
                       [("vpad", s) for s in (0, 1, 5)] + ["cb", "pt%d" % pb], [psn[st + which]])
        if tl == 3:
            t0 = t - 3
            rb = grp % 2
            S.op("vector", lambda e: e.tensor_tensor(out=dtmp[:], in0=ps[st + 1][:],
                                                     in1=sinkbc[:, t0:t0 + 4, :].rearrange("p t q -> p (t q)"),
                                                     op=ALU.add),
                 reads=[psn[st + 1], "sinkbc"], writes=["dtmp"])
            S.op("vector", lambda e: e.reciprocal(out=rec[rb][:], in_=dtmp[:]), reads=["dtmp"],
                 writes=["rec%d" % rb])
            S.op("vector", lambda e: e.tensor_tensor(
                out=yT[:, 8 + t0:8 + t0 + 4, n * 128:(n + 1) * 128],
                in0=ps[st][:].rearrange("p (t q) -> p t q", t=4),
                in1=rec[rb][:].rearrange("p (t q) -> p t q", t=4), op=ALU.mult),
                 reads=[psn[st], "rec%d" % rb], writes=[("yT", 8 + t0 + i, n) for i in range(4)])

    if KP:
        print("swa start", len(S.ops))
    for idx, (n, t) in enumerate(swa_units):
        if KP and idx < 10:
            print(" swa unit", idx, len(S.ops))
        if idx == 0:
            swa_scores(0, *swa_units[0])
            swa_scores(1, *swa_units[1])
        swa_pv(idx, n, t)
        if idx + 2 < len(swa_units):
            swa_scores(idx + 2, *swa_units[idx + 2])

    dump("yT", yT[:], [128, 16, T], BF16, [("yT", k, tt) for k in range(16) for tt in range(8)])
    if stop == 2:
        return finish()

    def load_wout(n):
        src = w_out.ap()[:, n * 512:(n + 1) * 512].rearrange("(k p) c -> p k c", p=128)
        wload(wout[n % 2][:], src, "wout%d" % (n % 2), "wout%d" % (n % 2))

    def load_wup(jblk):
        src = w_up.ap()[:, jblk * 512:(jblk + 1) * 512].rearrange("(k p) c -> p k c", p=128)
        wload(wup[jblk % 2][:], src, "wup%d" % (jblk % 2), "wup%d" % (jblk % 2))

    def load_wdn(jblk):
        src = w_dn.ap()[jblk * 512:(jblk + 1) * 512, :].rearrange("(k p) c -> p k c", p=128)
        wload(wdn[jblk % 2][:], src, "wdn%d" % (jblk % 2), "wdn%d" % (jblk % 2))

    load_wout(0)
    load_wout(1)
    for tt in range(NT):
        S.op("sync", lambda e, tt=tt: e.dma_start(out=h[:, tt, :], in_=x_own.ap()[tt * 128:(tt + 1) * 128, :]),
             writes=[("h", tt, n) for n in range(4)], dma="h%d" % tt)
    load_wdn(0)
    load_wdn(1)
    bankc = 0
    for n in range(4):
        for tt in range(NT):
            b = bankc % 8
            bankc += 1
            for k in range(16):
                mm(ps[b][:], yT[:, k, tt * 128:(tt + 1) * 128], wout[n % 2][:, k, :], k == 0, k == 15,
                   [("yT", k, tt), "wout%d" % (n % 2)], [psn[b]])
            S.op("vector", lambda e, b=b, tt=tt, n=n: e.tensor_tensor(
                out=h[:, tt, n * 512:(n + 1) * 512], in0=ps[b][:], in1=h[:, tt, n * 512:(n + 1) * 512],
                op=ALU.add),
                 reads=[psn[b], ("h", tt, n)], writes=[("h", tt, n)])
        if n + 2 < 4:
            load_wout(n + 2)
    dump("h1", h[:], [128, NT, D], F32, [("h", tt, n) for tt in range(NT) for n in range(4)])
    if stop == 3:
        return finish()

    S.op("sync", lambda e: e.dma_start(out=gbc[:], in_=bass.AP(g_mlp.get(), 0, [[0, 128], [1, D]])),
         writes=["gbc"], dma="gbc")
    load_wup(0)
    load_wup(1)
    h_tiles = [(h[:, tt, :], [("h", tt, n) for n in range(4)], None) for tt in range(NT)]
    norm_pass(h_tiles, mT, "mT", lambda j: None)
    S.op("sync", lambda e: e.dma_start(out=gbc[:], in_=bass.AP(g_fin.get(), 0, [[0, 128], [1, D]])),
         writes=["gbc"], dma="gbc")

    NB = HID // 512
    upbank = {"i": 0}
    dnbank = {"i": 0}
    UPB = [0, 1, 6, 7]
    DNB = [2, 3, 4, 5]

    def up_block(jblk):
        s = jblk % 2
        for m in range(4):
            for c in range(2):
                b = UPB[upbank["i"] % 4]
                upbank["i"] += 1
                for k in range(16):
                    mm(ps[b][:], wup[s][:, k, m * 128:(m + 1) * 128], mT[:, k, c * 512:(c + 1) * 512],
                       k == 0, k == 15, [("mT", 4 * c + i) for i in range(4)] + ["wup%d" % s], [psn[b]])
                rb = (m * 2 + c) % 2
                S.op("scalar", lambda e, b=b, rb=rb: e.activation(out=r32[rb][:], in_=ps[b][:], func=AF.Relu),
                     reads=[psn[b]], writes=["r32_%d" % rb])
                S.op("vector", lambda e, rb=rb, m=m, c=c: e.tensor_tensor(
                    out=hid[s][:, m, c * 512:(c + 1) * 512], in0=r32[rb][:], in1=r32[rb][:], op=ALU.mult),
                     reads=["r32_%d" % rb], writes=[("hid%d" % s, m, c)])

    def down_block(jblk):
        s = jblk % 2
        for tt in range(NT):
            for n in range(4):
                b = DNB[dnbank["i"] % 4]
                dnbank["i"] += 1
                for m in range(4):
                    mm(ps[b][:], hid[s][:, m, tt * 128:(tt + 1) * 128], wdn[s][:, m, n * 512:(n + 1) * 512],
                       m == 0, m == 3, [("hid%d" % s, m, tt // 4), "wdn%d" % s], [psn[b]])
                S.op("vector", lambda e, b=b, tt=tt, n=n: e.tensor_tensor(
                    out=h[:, tt, n * 512:(n + 1) * 512], in0=ps[b][:], in1=h[:, tt, n * 512:(n + 1) * 512],
                    op=ALU.add),
                     reads=[psn[b], ("h", tt, n)], writes=[("h", tt, n)])

    up_block(0)
    for jblk in range(NB):
        if jblk + 1 < NB:
            up_block(jblk + 1)
        if jblk + 2 < NB:
            load_wup(jblk + 2)
        down_block(jblk)
        if jblk + 2 < NB:
            load_wdn(jblk + 2)

    for tt in range(NT):
        col = state["col"]
        state["col"] += 1
        regs = [("h", tt, n) for n in range(4)]
        norm_stats(h[:, tt, :], regs, col)
        ob = tt % 2
        S.op("vector", lambda e, tt=tt, col=col, ob=ob: e.scalar_tensor_tensor(
            out=ostage[ob][:], in0=h[:, tt, :], scalar=rstd[:, col:col + 1], in1=gbc[:],
            op0=ALU.mult, op1=ALU.mult),
             reads=regs + [("rstd", col), "gbc"], writes=["ostage%d" % ob])
        S.op("sync", lambda e, tt=tt, ob=ob: e.dma_start(out=out_d.ap()[tt * 128:(tt + 1) * 128, :],
                                                         in_=ostage[ob][:]),
             reads=["ostage%d" % ob], dma="ostage%d" % ob)

    return finish()


def _t5_bucket_const(d):
    d = np.asarray(d)
    n = np.maximum(d, 0)
    max_exact = 16
    n_safe = np.maximum(n, 1).astype(np.float32)
    large = max_exact + (np.log(n_safe / max_exact) / math.log(128 / max_exact) * (32 - max_exact)).astype(np.int32)
    large = np.minimum(large, 31)
    return np.where(n < max_exact, n, large)


def _consts():
    bf = ml_dtypes.bfloat16
    cb = np.zeros((128, CB_N), dtype=np.float32)
    cb[:, CB_ID:CB_ID + 128] = np.eye(128)
    cb[:, CB_ID8:CB_ID8 + 128] = 8.0 * np.eye(128)
    cb[:, CB_ONES:CB_ONES + 128] = 1.0
    cb[:, CB_OP0:CB_OP0 + 64] = 1.0
    cb[:, CB_OP1 + 64:CB_OP1 + 128] = 1.0
    k = np.arange(128)[:, None]
    i = np.arange(512)[None, :]
    for j in range(4):
        cb[:, CB_STAIR + 512 * j:CB_STAIR + 512 * (j + 1)] = np.where(k + 128 * j > i, NEG, 0.0)
    return cb.astype(bf)


_CACHE = {}


def kernel(x, positions, rel_bias, attn_norm, w_in, q_a_norm, w_q_b, kv_a_norm, w_kv_b,
           sinks, w_out, mlp_norm, w_up, w_down, final_norm):
    if "nc" not in _CACHE:
        _CACHE["nc"] = build_program()
    nc, S = _CACHE["nc"]
    in_maps = make_in_maps(x, positions, rel_bias, attn_norm, w_in, q_a_norm, w_q_b, kv_a_norm, w_kv_b,
                           sinks, w_out, mlp_norm, w_up, w_down, final_norm)
    res = run_bass_kernel_spmd(nc, in_maps, core_ids=list(range(8)))
    out = np.empty((4, 2048, D), dtype=np.float32)
    for c in range(8):
        b, half = c // 2, c % 2
        out[b, half * T:(half + 1) * T] = np.asarray(res.results[c]["out"], dtype=np.float32)
    if DEBUG:
        _CACHE["dbg"] = [{k: np.asarray(v) for k, v in r.items() if k.startswith("dbg_")} for r in res.results]
    return out


def make_in_maps(x, positions, rel_bias, attn_norm, w_in, q_a_norm, w_q_b, kv_a_norm, w_kv_b,
                 sinks, w_out, mlp_norm, w_up, w_down, final_norm):
    x = np.asarray(x, dtype=np.float32)
    positions = np.asarray(positions, dtype=np.int32)
    rel_bias = np.asarray(rel_bias, dtype=np.float32)

    cbv = _consts()
    kk = np.arange(128)[:, None]
    qq = np.arange(128)[None, :]
    swab = np.empty((128, 16, 2, 128), dtype=np.float32)
    for kb in range(2):
        dist = qq - kk + (128 if kb == 0 else 0)
        valid = (dist >= 0) & (dist < 128)
        bidx = _t5_bucket_const(np.clip(dist, 0, 127))
        gathered = rel_bias[bidx]
        sel = np.where(valid[:, :, None], gathered, np.float32(NEG))
        swab[:, :, kb, :] = np.transpose(sel, (0, 2, 1))
    swab = np.ascontiguousarray(swab.reshape(128, 4096))

    inv = (1.0 / (10000.0 ** (np.arange(0, 64, 2, dtype=np.float32) / 64.0))).astype(np.float32)
    p = np.arange(128)
    cf_base = np.zeros((128, CF_N), dtype=np.float32)
    cf_base[:, CF_INV] = (inv[p % 32].astype(np.float64) / (2.0 * math.pi)).astype(np.float32)
    cf_base[:, CF_SGN] = np.where((p % 64) < 32, -1.0, 1.0)
    cf_base[:, CF_GQ:CF_GQ + 4] = np.asarray(q_a_norm, np.float32)[0].reshape(4, 128).T
    cf_base[:, CF_GKV:CF_GKV + 4] = np.asarray(kv_a_norm, np.float32)[0].reshape(4, 128).T
    sk = np.asarray(sinks, np.float32)[0]
    for t in range(8):
        cf_base[:64, CF_SINK + t] = sk[2 * t]
        cf_base[64:, CF_SINK + t] = sk[2 * t + 1]

    common = {
        "cb": cbv, "swab": swab,
        "g_attn": np.ascontiguousarray(np.asarray(attn_norm, np.float32)[0]),
        "g_mlp": np.ascontiguousarray(np.asarray(mlp_norm, np.float32)[0]),
        "g_fin": np.ascontiguousarray(np.asarray(final_norm, np.float32)),
        "w_in": np.ascontiguousarray(np.asarray(w_in, np.float32)[0]),
        "w_qb": np.ascontiguousarray(np.asarray(w_q_b, np.float32)[0]),
        "w_kvb": np.ascontiguousarray(np.asarray(w_kv_b, np.float32)[0]),
        "w_out": np.ascontiguousarray(np.asarray(w_out, np.float32)[0]),
        "w_up": np.ascontiguousarray(np.asarray(w_up, np.float32)[0]),
        "w_dn": np.ascontiguousarray(np.asarray(w_down, np.float32)[0]),
    }
    in_maps = []
    for c in range(8):
        b, half = c // 2, c % 2
        own = slice(half * T, (half + 1) * T)
        oth = slice((1 - half) * T, (2 - half) * T)
        cf = cf_base.copy()
        cf[:, CF_OTH] = 0.0 if half == 1 else NEG
        m = dict(common)
        m["x_own"] = np.ascontiguousarray(x[b, own])
        m["x_oth"] = np.ascontiguousarray(x[b, oth])
        m["pos_own"] = np.ascontiguousarray(positions[b, own])
        m["pos_oth"] = np.ascontiguousarray(positions[b, oth])
        m["cf"] = cf
        in_maps.append(m)
    return in_maps
```
